# Optimizing a Trainium2 kernel written in Bass

```python
import jax, jax.numpy as jnp
from jax import lax
import numpy as np

D_MODEL = 4096
BATCH = 2
SEQ = 8192
DEPTH = 4

CHUNK = 64
N_META = 16
NORM_EPS = 1e-6
D_FF = 4096
HG_HEADS = 16
HG_DK = 128
HG_DV = 128
HG_WIDTH = HG_HEADS * HG_DK
HG_VWIDTH = HG_HEADS * HG_DV
ML_HEADS = 8
ML_DK = 128
ML_DV = 256
ML_QK = ML_HEADS * ML_DK
ML_V = ML_HEADS * ML_DV
CONV_W = 4
GATE_CAP = 15.0
IN_SIZES = (HG_WIDTH, HG_WIDTH, HG_VWIDTH, HG_VWIDTH,
            ML_QK, ML_QK, ML_V, ML_V, ML_HEADS, ML_HEADS,
            D_MODEL, D_MODEL)
IN_COLS = sum(IN_SIZES)

kernel_name = "hgrn2_mlstm_macaron_meta_hybrid"


def rmsnorm(x, g):
    xf = x.astype(jnp.float32)
    y = xf * lax.rsqrt(jnp.mean(xf * xf, axis=-1, keepdims=True) + NORM_EPS)
    return (y * g.astype(jnp.float32)).astype(x.dtype)


def head_rmsnorm(o, g):
    y = o * lax.rsqrt(jnp.mean(o * o, axis=-1, keepdims=True) + NORM_EPS)
    return y * g.astype(jnp.float32).reshape(o.shape[-2:])


def swiglu(x, w_gate, w_up, w_down):
    return (jax.nn.silu(x @ w_gate) * (x @ w_up)) @ w_down


def soft_cap(a):
    return GATE_CAP * jnp.tanh(a / GATE_CAP)


def to_chunks(a, pad_value):
    pad = (-a.shape[1]) % CHUNK
    widths = [(0, 0)] * a.ndim
    widths[1] = (pad, 0)
    a = jnp.pad(a, widths, constant_values=pad_value)
    nc = a.shape[1] // CHUNK
    a = a.reshape((a.shape[0], nc, CHUNK) + a.shape[2:])
    return jnp.moveaxis(jnp.moveaxis(a, 1, 0), 3, 2)


def from_chunks(a, L):
    a = jnp.moveaxis(jnp.moveaxis(a, 2, 3), 0, 1)
    a = a.reshape((a.shape[0], -1) + a.shape[3:])
    return a[:, a.shape[1] - L:]


def causal_dwconv(x, w, b):
    y = lax.conv_general_dilated(x, w[:, None, :].astype(x.dtype), window_strides=(1,),
                                 padding=[(CONV_W - 1, 0)],
                                 dimension_numbers=("NWC", "WIO", "NWC"),
                                 feature_group_count=x.shape[-1])
    return y + b.astype(x.dtype)


def hgrn2_branch(q, f_logit, v, gate, lb, norm_g):
    B, L, _ = q.shape
    f32 = jnp.float32
    q = jax.nn.silu(q.astype(f32)).reshape(B, L, HG_HEADS, HG_DK)
    z = f_logit.astype(f32)
    log_f = jnp.logaddexp(jnp.log(lb), jnp.log1p(-lb) + jax.nn.log_sigmoid(z))
    k = (-jnp.expm1(log_f)).reshape(B, L, HG_HEADS, HG_DK)
    log_f = log_f.reshape(B, L, HG_HEADS, HG_DK)
    v = v.astype(f32).reshape(B, L, HG_HEADS, HG_DV)
    qc, kc, vc, gc = to_chunks(q, 0.0), to_chunks(k, 0.0), to_chunks(v, 0.0), to_chunks(log_f, 0.0)
    causal = jnp.tril(jnp.ones((CHUNK, CHUNK), dtype=bool))[:, :, None]

    def step(S, inp):
        qb, kb, vb, gb = inp
        b = jnp.cumsum(gb, axis=2)
        o_inter = jnp.einsum("bhik,bhkv->bhiv", qb * jnp.exp(b), S)
        diff = b[:, :, :, None, :] - b[:, :, None, :, :]
        decay = jnp.exp(jnp.where(causal, diff, -jnp.inf))
        A = jnp.einsum("bhik,bhjk,bhijk->bhij", qb, kb, decay)
        o = o_inter + jnp.einsum("bhij,bhjv->bhiv", A, vb)
        b_last = b[:, :, -1:, :]
        S_new = jnp.exp(b_last[:, :, 0, :])[..., None] * S + jnp.einsum(
            "bhjk,bhjv->bhkv", kb * jnp.exp(b_last - b), vb)
        return S_new, o

    S0 = jnp.zeros((B, HG_HEADS, HG_DK, HG_DV), f32)
    _, o = lax.scan(step, S0, (qc, kc, vc, gc))
    o = from_chunks(o, L)
    o = head_rmsnorm(o, norm_g) * jax.nn.silu(gate.astype(f32)).reshape(B, L, HG_HEADS, HG_DV)
    return o.reshape(B, L, HG_VWIDTH).astype(gate.dtype)


def mlstm_branch(q, k, v, o_logit, i_logit, f_logit, conv_w, conv_b, i_bias, f_bias, norm_g):
    B, L, _ = q.shape
    f32 = jnp.float32
    qk = jax.nn.silu(causal_dwconv(jnp.concatenate([q, k], axis=-1), conv_w, conv_b)).astype(f32)
    q = qk[..., :ML_QK].reshape(B, L, ML_HEADS, ML_DK)
    k = (qk[..., ML_QK:] * (ML_DK ** -0.5)).reshape(B, L, ML_HEADS, ML_DK)
    v = v.astype(f32).reshape(B, L, ML_HEADS, ML_DV)
    i_pre = soft_cap(i_logit.astype(f32) + i_bias.astype(f32))
    log_f = jax.nn.log_sigmoid(soft_cap(f_logit.astype(f32) + f_bias.astype(f32)))
    qc, kc, vc = to_chunks(q, 0.0), to_chunks(k, 0.0), to_chunks(v, 0.0)
    ic, fc = to_chunks(i_pre, -jnp.inf), to_chunks(log_f, 0.0)
    causal = jnp.tril(jnp.ones((CHUNK, CHUNK), dtype=bool))

    def step(carry, inp):
        Cs, ns, m = carry
        qb, kb, vb, ib, fb = inp
        F = jnp.cumsum(fb, axis=-1)
        logD = jnp.where(causal, F[..., :, None] - F[..., None, :] + ib[..., None, :], -jnp.inf)
        log_prev = F + m[..., None]
        m_t = jnp.maximum(log_prev, jnp.max(logD, axis=-1))
        w_prev = jnp.exp(log_prev - m_t)
        Sqk = jnp.einsum("bhik,bhjk->bhij", qb, kb) * jnp.exp(logD - m_t[..., None])
        num = w_prev[..., None] * jnp.einsum("bhik,bhkv->bhiv", qb, Cs) + jnp.einsum("bhij,bhjv->bhiv", Sqk, vb)
        den = w_prev * jnp.einsum("bhik,bhk->bhi", qb, ns) + jnp.sum(Sqk, axis=-1)
        h = num / jnp.maximum(jnp.abs(den), jnp.exp(-m_t))[..., None]
        m_new = m_t[..., -1]
        w_old = jnp.exp(F[..., -1] + m - m_new)
        w_in = jnp.exp(F[..., -1:] - F + ib - m_new[..., None])
        Cs_new = w_old[..., None, None] * Cs + jnp.einsum("bhj,bhjk,bhjv->bhkv", w_in, kb, vb)
        ns_new = w_old[..., None] * ns + jnp.einsum("bhj,bhjk->bhk", w_in, kb)
        return (Cs_new, ns_new, m_new), h

    init = (jnp.zeros((B, ML_HEADS, ML_DK, ML_DV), f32),
            jnp.zeros((B, ML_HEADS, ML_DK), f32),
            jnp.zeros((B, ML_HEADS), f32))
    _, h = lax.scan(step, init, (qc, kc, vc, ic, fc))
    h = from_chunks(h, L)
    h = head_rmsnorm(h, norm_g) * jax.nn.sigmoid(o_logit.astype(f32)).reshape(B, L, ML_HEADS, ML_DV)
    return h.reshape(B, L, ML_V).astype(o_logit.dtype)


def mixer_block(u, w_in, lb, conv_w, conv_b, i_bias, f_bias, hg_norm, ml_norm, w_a, w_b, w_o):
    z = u @ w_in
    offs = np.cumsum(IN_SIZES)[:-1].tolist()
    hq, hf, hi, hg, mq, mk, mv, mo, mi, mf, ga, gb = jnp.split(z, offs, axis=-1)
    ya = hgrn2_branch(hq, hf, hi, hg, lb, hg_norm)
    yb = mlstm_branch(mq, mk, mv, mo, mi, mf, conv_w, conv_b, i_bias, f_bias, ml_norm)
    y = jax.nn.sigmoid(ga) * (ya @ w_a) + jax.nn.sigmoid(gb) * (yb @ w_b)
    return y @ w_o


def setup_inputs(seed: int = 0) -> dict:
    key = jax.random.key(seed)
    ks = jax.random.split(key, 24)
    f32 = jnp.float32

    def dense(k, shape, fan_in):
        return jax.random.normal(k, shape, f32) * (fan_in ** -0.5)

    def gain(k, shape):
        return 1.0 + 0.01 * jax.random.normal(k, shape, f32)

    return {
        "x": jax.random.normal(ks[0], (BATCH, SEQ, D_MODEL), f32),
        "meta_tokens": jax.random.normal(ks[1], (N_META, D_MODEL), f32),
        "hgrn_lb_logits": 0.5 * jax.random.normal(ks[2], (DEPTH, HG_WIDTH), f32),
        "norm_ffn1": gain(ks[3], (DEPTH, D_MODEL)),
        "ffn1_w_gate": dense(ks[4], (DEPTH, D_MODEL, D_FF), D_MODEL),
        "ffn1_w_up": dense(ks[5], (DEPTH, D_MODEL, D_FF), D_MODEL),
        "ffn1_w_down": dense(ks[6], (DEPTH, D_FF, D_MODEL), D_FF),
        "norm_mix": gain(ks[7], (DEPTH, D_MODEL)),
        "w_in": dense(ks[8], (DEPTH, D_MODEL, IN_COLS), D_MODEL),
        "mlstm_conv_w": dense(ks[9], (DEPTH, CONV_W, 2 * ML_QK), CONV_W),
        "mlstm_conv_b": 0.01 * jax.random.normal(ks[10], (DEPTH, 2 * ML_QK), f32),
        "mlstm_igate_b": 0.1 * jax.random.normal(ks[11], (DEPTH, ML_HEADS), f32),
        "mlstm_fgate_b": jnp.linspace(3.0, 6.0, ML_HEADS, dtype=f32)[None, :]
                         + 0.1 * jax.random.normal(ks[12], (DEPTH, ML_HEADS), f32),
        "hgrn_out_norm": gain(ks[13], (DEPTH, HG_VWIDTH)),
        "mlstm_out_norm": gain(ks[14], (DEPTH, ML_V)),
        "w_branch_a": dense(ks[15], (DEPTH, HG_VWIDTH, D_MODEL), HG_VWIDTH),
        "w_branch_b": dense(ks[16], (DEPTH, ML_V, D_MODEL), ML_V),
        "w_out": dense(ks[17], (DEPTH, D_MODEL, D_MODEL), D_MODEL),
        "norm_ffn2": gain(ks[18], (DEPTH, D_MODEL)),
        "ffn2_w_gate": dense(ks[19], (DEPTH, D_MODEL, D_FF), D_MODEL),
        "ffn2_w_up": dense(ks[20], (DEPTH, D_MODEL, D_FF), D_MODEL),
        "ffn2_w_down": dense(ks[21], (DEPTH, D_FF, D_MODEL), D_FF),
        "final_norm": gain(ks[22], (D_MODEL,)),
    }


def reference(x, meta_tokens, hgrn_lb_logits, norm_ffn1, ffn1_w_gate, ffn1_w_up, ffn1_w_down,
              norm_mix, w_in, mlstm_conv_w, mlstm_conv_b, mlstm_igate_b, mlstm_fgate_b,
              hgrn_out_norm, mlstm_out_norm, w_branch_a, w_branch_b, w_out,
              norm_ffn2, ffn2_w_gate, ffn2_w_up, ffn2_w_down, final_norm):
    B = x.shape[0]
    meta = jnp.broadcast_to(meta_tokens[None].astype(x.dtype), (B, N_META, D_MODEL))
    h = jnp.concatenate([meta, x], axis=1)
    lb_all = jnp.cumsum(jax.nn.softmax(hgrn_lb_logits.astype(jnp.float32), axis=0), axis=0)
    lb_all = lb_all - lb_all[0:1]
    for l in range(DEPTH):
        h = h + 0.5 * swiglu(rmsnorm(h, norm_ffn1[l]), ffn1_w_gate[l], ffn1_w_up[l], ffn1_w_down[l])
        h = h + mixer_block(rmsnorm(h, norm_mix[l]), w_in[l], lb_all[l], mlstm_conv_w[l], mlstm_conv_b[l],
                            mlstm_igate_b[l], mlstm_fgate_b[l], hgrn_out_norm[l], mlstm_out_norm[l],
                            w_branch_a[l], w_branch_b[l], w_out[l])
        h = h + 0.5 * swiglu(rmsnorm(h, norm_ffn2[l]), ffn2_w_gate[l], ffn2_w_up[l], ffn2_w_down[l])
    h = rmsnorm(h, final_norm)
    return h[:, N_META:]
```

```python
import contextlib
import numpy as np
import concourse.bass as bass
import concourse.mybir as mybir
from concourse.bass_utils import run_bass_kernel_spmd

F32 = mybir.dt.float32
BF16 = mybir.dt.bfloat16
ALU = mybir.AluOpType
AF = mybir.ActivationFunctionType

D = 4096
NKC = 32
EPS = 1e-6
NMETA = 16
GROUPS4 = [[0, 1, 2, 3], [4, 5, 6, 7]]
GROUPS8 = [[0, 1, 2, 3, 4, 5, 6, 7]]
B_F1G, B_F1U, B_F1D, B_WIN, B_WO, B_F2G, B_F2U, B_F2D = 0, 32, 64, 96, 273, 305, 337, 369
NS4 = 401
NS2 = 64
GRP_ROWS = 3200
SB2_ROWS = 1024
C_ID, C_SEL, C_MASKT, C_SEG, C_ONES, C_SELG = 0, 128, 1152, 1216, 1728, 1856
NCONST = 1888


def prm_layout(L):
    o = {}
    c = 0
    for name, n in (("nf1", 32 * L), ("nmix", 32 * L), ("nf2", 32 * L), ("nfin", 32), ("lbl", 16 * L),
                    ("cw", 16 * L), ("cb", 4 * L), ("gb", L), ("hgn", 4 * L), ("mln", 4 * L)):
        o[name] = c
        c += n
    o["_n"] = c
    return o


class Reg:
    __slots__ = ("w", "r")

    def __init__(self):
        self.w = None
        self.r = {}


class Sched:
    NRING = 12

    def __init__(self, nc, es):
        self.nc = nc
        self.engs = {"pe": nc.tensor, "act": nc.scalar, "dve": nc.vector, "pool": nc.gpsimd, "sp": nc.sync}
        self.sems = {}
        self.cnt = {}
        self.seen = {e: {} for e in self.engs}
        for e in self.engs:
            self.sems[e] = es.enter_context(nc.semaphore("s_" + e))
            self.cnt[e] = 0
        self.ring = {}
        self.ring_i = {}
        for q in ("sp", "pool"):
            self.ring[q] = []
            for i in range(self.NRING):
                k = "d_%s_%d" % (q, i)
                self.sems[k] = es.enter_context(nc.semaphore(k))
                self.cnt[k] = 0
                self.ring[q].append(k)
            self.ring_i[q] = 0
        self.ncc = 0
        self.es = es
        import os
        self.maxops = int(os.environ.get("MK_MAXOPS", "1000000000"))
        self.nops = 0

    def _skip(self):
        self.nops += 1
        return self.nops > self.maxops

    def wait(self, e, ev):
        k, v = ev
        if k == e and e == "pe":
            return
        if self.seen[e].get(k, 0) >= v:
            return
        self.engs[e].wait_ge(self.sems[k], v)
        self.seen[e][k] = v

    def _deps(self, e, reads, writes):
        evs = set()
        for r in reads:
            if r.w is not None:
                evs.add(r.w)
        for w in writes:
            if w.w is not None:
                evs.add(w.w)
            for ev in w.r.values():
                evs.add(ev)
        for ev in sorted(evs):
            self.wait(e, ev)

    def _reg(self, ev, reads, writes):
        for r in reads:
            r.r[ev[0]] = ev
        for w in writes:
            w.w = ev
            w.r = {}

    def op(self, e, fn, reads=(), writes=()):
        if self._skip():
            return None
        self._deps(e, reads, writes)
        ins = fn(self.engs[e])
        self.cnt[e] += 1
        ins.then_inc(self.sems[e], 1)
        ev = (e, self.cnt[e])
        self._reg(ev, reads, writes)
        return ev

    def mm(self, out_ap, out_reg, parts, reads):
        if self._skip():
            return None
        self._deps("pe", reads, [out_reg])
        n = len(parts)
        ins = None
        for i, (l, r) in enumerate(parts):
            ins = self.nc.tensor.matmul(out_ap, l, r, start=(i == 0), stop=(i == n - 1))
        self.cnt["pe"] += 1
        ins.then_inc(self.sems["pe"], 1)
        ev = ("pe", self.cnt["pe"])
        self._reg(ev, reads, [out_reg])
        return ev

    def transpose(self, out_ap, out_reg, in_ap, ident_ap, reads):
        if self._skip():
            return None
        self._deps("pe", reads, [out_reg])
        ins = self.nc.tensor.transpose(out_ap, in_ap, ident_ap)
        self.cnt["pe"] += 1
        ins.then_inc(self.sems["pe"], 1)
        ev = ("pe", self.cnt["pe"])
        self._reg(ev, reads, [out_reg])
        return ev

    def dma(self, q, out_ap, in_ap, reads=(), writes=()):
        if self._skip():
            return None
        self._deps(q, reads, writes)
        k = self.ring[q][self.ring_i[q] % self.NRING]
        self.ring_i[q] += 1
        if self.cnt[k] > 0:
            self.wait(q, (k, self.cnt[k]))
        ins = self.engs[q].dma_start(out=out_ap, in_=in_ap)
        self.cnt[k] += 16
        ins.then_inc(self.sems[k], 16)
        ev = (k, self.cnt[k])
        self._reg(ev, reads, writes)
        return ev

    def allgather(self, groups, in_ap, out_ap, reads, writes):
        if self._skip():
            return None
        self._deps("pool", reads, writes)
        k = "cc_%d" % self.ncc
        self.ncc += 1
        self.sems[k] = self.es.enter_context(self.nc.semaphore(k))
        ins = self.nc.gpsimd.collective_compute("AllGather", ALU.bypass, replica_groups=groups,
                                                ins=[in_ap], outs=[out_ap])
        ins.then_inc(self.sems[k], 1)
        self.cnt[k] = 1
        ev = (k, 1)
        self._reg(ev, reads, writes)
        return ev

    def barrier(self):
        evs = [(k, v) for k, v in self.cnt.items() if v > 0]
        for e in self.engs:
            for ev in sorted(evs):
                if ev[0] == e:
                    continue
                self.wait(e, ev)


def build_nc(L, NT):
    import os as _os
    TOK = NT * 512
    NC_ = NMETA + TOK
    NSEQT = 4 * NT
    PL = prm_layout(L)
    nc = bass.Bass("TRN2", target_bir_lowering=False)
    es = contextlib.ExitStack()
    with es:
        S = Sched(nc, es)

        def dram_in(name, shape, dt=F32):
            return nc.dram_tensor(name, shape, dt, kind="ExternalInput").ap()

        def dram(name, shape, dt):
            return nc.dram_tensor(name, shape, dt).ap()

        xin = dram_in("xin", [128, NKC, NC_])
        consts_d = dram_in("consts", [128, NCONST])
        prm_d = dram_in("prm", [128, PL["_n"]])
        w4_in = [dram_in("w4_%d" % l, [NS4 * 128, 512]) for l in range(L)]
        w2_in = [dram_in("w2_%d" % l, [NS2 * 128, 256]) for l in range(L)]
        out_d = nc.dram_tensor("out", [128, NKC, TOK], F32, kind="ExternalOutput").ap()

        w4_b = [dram("w4b_%d" % l, [NS4 * 128, 512], BF16) for l in range(L)]
        w2_b = [dram("w2b_%d" % l, [NS2 * 128, 256], BF16) for l in range(L)]
        NSA = 200
        NSB = NS4 - NSA
        w4_gA = [dram("w4gA_%d" % l, [8 * NSA * 128, 512], BF16) for l in range(L)]
        w4_gB = [dram("w4gB_%d" % l, [8 * NSB * 128, 512], BF16) for l in range(L)]
        w2_g = [dram("w2g_%d" % l, [8 * NS2 * 128, 256], BF16) for l in range(L)]
        w4_reg = [(Reg(), Reg()) for _ in range(L)]
        w2_reg = [Reg() for _ in range(L)]
        xs = dram("xs", [128, NKC, NC_], F32)
        lg = dram("lg", [96 * 128, NC_], BF16)
        sb1 = dram("sb1", [4 * GRP_ROWS, NC_], BF16)
        g1t = [dram("g1_%d" % j, [8 * GRP_ROWS, NC_], BF16) for j in range(4)]
        sb2 = dram("sb2", [4 * SB2_ROWS, NC_], BF16)
        g2 = dram("g2", [32 * SB2_ROWS, NC_], BF16)
        tiles = [(0, NMETA)] + [(NMETA + 512 * i, 512) for i in range(NT)]
        xs_reg = [Reg() for _ in tiles]
        lg_reg = [[] for _ in tiles]
        sb1_reg, sb2_reg = [], []
        g1_regs = [Reg() for _ in range(4)]
        g2_reg = Reg()

        for l in range(L):
            npc = 8
            rows = NS4 * 128
            step = (rows + npc - 1) // npc
            pieces = [Reg() for _ in range(npc)]
            for i in range(npc):
                r0, r1 = i * step, min(rows, (i + 1) * step)
                S.dma("pool", w4_b[l][r0:r1, :], w4_in[l][r0:r1, :], writes=[pieces[i]])
            S.allgather(GROUPS8, w4_b[l][0:NSA * 128, :], w4_gA[l][:, :], reads=pieces, writes=[w4_reg[l][0]])
            S.allgather(GROUPS8, w4_b[l][NSA * 128:NS4 * 128, :], w4_gB[l][:, :], reads=pieces, writes=[w4_reg[l][1]])
            t2 = Reg()
            S.dma("pool", w2_b[l][:, :], w2_in[l][:, :], writes=[t2])
            S.allgather(GROUPS8, w2_b[l][:, :], w2_g[l][:, :], reads=[t2], writes=[w2_reg[l]])

        CT = es.enter_context(nc.sbuf_tensor("consts_f", [128, NCONST], F32))
        CB = es.enter_context(nc.sbuf_tensor("consts_b", [128, NCONST], BF16))
        PR = es.enter_context(nc.sbuf_tensor("prm_s", [128, PL["_n"]], F32))
        LB = es.enter_context(nc.sbuf_tensor("lb_s", [128, 2 * 16 * L + 64], F32))
        GBS = es.enter_context(nc.sbuf_tensor("gbs", [16, L], F32))
        WR = [es.enter_context(nc.sbuf_tensor("wr%d" % i, [128, 8, 512], BF16)) for i in range(4)]
        WR_reg = [Reg() for _ in range(4)]
        PS = [es.enter_context(nc.psum_tensor("ps%d" % i, [128, 512], F32)) for i in range(7)]
        PS_reg = [Reg() for _ in range(7)]
        PT = es.enter_context(nc.psum_tensor("pst", [128, 1024], BF16))
        PT_reg = Reg()
        c_reg, p_reg = Reg(), Reg()
        S.dma("sp", CT[:, :], consts_d[:, :], writes=[c_reg])
        S.dma("sp", PR[:, :], prm_d[:, :], writes=[p_reg])
        S.op("dve", lambda e: e.tensor_copy(out=CB[:, :], in_=CT[:, :]), reads=[c_reg], writes=[c_reg])
        EPSC = es.enter_context(nc.sbuf_tensor("epsc", [128, 1], F32))
        S.op("pool", lambda e: e.memset(EPSC[:, :], EPS), writes=[c_reg])
        ident = CB[:, C_ID:C_ID + 128]
        ones_b = CB[:, C_ONES:C_ONES + 128]
        ones_f = CT[:, C_ONES:C_ONES + 128]

        def selI(j):
            return CB[:, C_SEL + 128 * j:C_SEL + 128 * (j + 1)]

        S.dma("sp", xs[:, :, :], xin[:, :, :], writes=xs_reg)

        lbl = PL["lbl"]
        T0 = 2 * 16 * L
        S.op("act", lambda e: e.activation(out=PR[:, lbl:lbl + 16 * L], in_=PR[:, lbl:lbl + 16 * L], func=AF.Exp),
             reads=[p_reg], writes=[p_reg])
        S.op("dve", lambda e: e.tensor_copy(out=LB[:, T0:T0 + 16], in_=PR[:, lbl:lbl + 16]), reads=[p_reg], writes=[p_reg])
        for l in range(1, L):
            S.op("dve", lambda e, l=l: e.tensor_tensor(out=LB[:, T0:T0 + 16], in0=LB[:, T0:T0 + 16],
                                                       in1=PR[:, lbl + 16 * l:lbl + 16 * (l + 1)], op=ALU.add),
                 reads=[p_reg], writes=[p_reg])
        S.op("dve", lambda e: e.reciprocal(out=LB[:, T0 + 16:T0 + 32], in_=LB[:, T0:T0 + 16]), reads=[p_reg], writes=[p_reg])
        S.op("dve", lambda e: e.memset(LB[:, 0:16], 0.0), reads=[p_reg], writes=[p_reg])
        for l in range(1, L):
            S.op("dve", lambda e, l=l: e.tensor_tensor(out=LB[:, T0 + 32:T0 + 48], in0=PR[:, lbl + 16 * l:lbl + 16 * (l + 1)],
                                                       in1=LB[:, T0 + 16:T0 + 32], op=ALU.mult), reads=[p_reg], writes=[p_reg])
            S.op("dve", lambda e, l=l: e.tensor_tensor(out=LB[:, 16 * l:16 * (l + 1)], in0=LB[:, 16 * (l - 1):16 * l],
                                                       in1=LB[:, T0 + 32:T0 + 48], op=ALU.add), reads=[p_reg], writes=[p_reg])
        S.op("dve", lambda e: e.tensor_scalar(out=LB[:, 16 * L:2 * 16 * L], in0=LB[:, 0:16 * L], scalar1=-1.0, scalar2=1.0,
                                              op0=ALU.mult, op1=ALU.add), reads=[p_reg], writes=[p_reg])
        gbc = PL["gb"]
        S.op("dve", lambda e: e.tensor_scalar(out=GBS[:, :], in0=PR[0:16, gbc:gbc + L], scalar1=1.0 / 15.0, scalar2=None,
                                              op0=ALU.mult), reads=[p_reg], writes=[p_reg])

        plan = []
        for l in range(L + 1):
            for _ in tiles:
                if l > 0:
                    for m in range(32):
                        plan += [(l - 1, 2, m), (l - 1, 2, 32 + m)]
                    plan += [(l - 1, 4, B_WO + m) for m in range(32)]
                    for m in range(32):
                        plan += [(l - 1, 4, B_F2G + m), (l - 1, 4, B_F2U + m)]
                    plan += [(l - 1, 4, B_F2D + m) for m in range(32)]
                if l < L:
                    for m in range(32):
                        plan += [(l, 4, B_F1G + m), (l, 4, B_F1U + m)]
                    plan += [(l, 4, B_F1D + m) for m in range(32)]
                    plan += [(l, 4, B_WIN + m) for m in range(177)]
        wst = {"loaded": 0, "used": 0}

        def wget(spec):
            assert plan[wst["used"]] == spec, (plan[wst["used"]], spec)
            while wst["loaded"] < min(len(plan), wst["used"] + 3):
                i = wst["loaded"]
                l, k, m = plan[i]
                if k == 4:
                    if m < NSA:
                        src = w4_gA[l].rearrange("(r m p) c -> m p r c", r=8, m=NSA)[m]
                    else:
                        src = w4_gB[l].rearrange("(r m p) c -> m p r c", r=8, m=NSB)[m - NSA]
                    S.dma("sp", WR[i % 4][:, :, :], src, reads=[w4_reg[l][0 if m < NSA else 1]], writes=[WR_reg[i % 4]])
                else:
                    src = w2_g[l].rearrange("(r m p) c -> m p r c", r=8, m=NS2)[m]
                    S.dma("sp", WR[i % 4][:, :, 0:256], src, reads=[w2_reg[l]], writes=[WR_reg[i % 4]])
                wst["loaded"] += 1
            i = wst["used"]
            wst["used"] += 1
            return WR[i % 4], WR_reg[i % 4], spec[1]

        def wslice(wt, k, kc):
            if k == 4:
                return wt[:, kc // 4, (kc % 4) * 128:(kc % 4) * 128 + 128]
            return wt[:, kc // 2, (kc % 2) * 128:(kc % 2) * 128 + 128]

        psi = {"i": 0}

        def psn():
            i = psi["i"] % 7
            psi["i"] += 1
            return PS[i], PS_reg[i]

        def token_pass(l):
            with contextlib.ExitStack() as ts:
                X = ts.enter_context(nc.sbuf_tensor("X_%d" % l, [128, NKC, 512], F32))
                U = ts.enter_context(nc.sbuf_tensor("U_%d" % l, [128, NKC, 512], BF16))
                H = ts.enter_context(nc.sbuf_tensor("H_%d" % l, [128, NKC, 512], BF16))
                Rs = ts.enter_context(nc.sbuf_tensor("Rs_%d" % l, [128, 512], F32))
                TF = [ts.enter_context(nc.sbuf_tensor("TF%d_%d" % (i, l), [128, 512], F32)) for i in range(3)]
                ZB = [ts.enter_context(nc.sbuf_tensor("ZB%d_%d" % (i, l), [128, 512], BF16)) for i in range(4)]
                CA = [ts.enter_context(nc.sbuf_tensor("CAc%d_%d" % (i, l), [128, 8, 512], BF16)) for i in range(2)]
                GA = [ts.enter_context(nc.sbuf_tensor("GA%d_%d" % (i, l), [128, 2, 512], BF16)) for i in range(2)]
                Xr = [Reg() for _ in range(NKC)]
                Ur = [Reg() for _ in range(NKC)]
                Hr = [Reg() for _ in range(NKC)]
                Rr = Reg()
                TFr = [Reg() for _ in range(3)]
                ZBr = [Reg() for _ in range(4)]
                CAr = [Reg(), Reg()]
                GAr = [Reg(), Reg()]
                rot = {"tf": 0, "zb": 0, "ca": 0, "ga": 0}

                def tf():
                    i = rot["tf"] % 3
                    rot["tf"] += 1
                    return TF[i], TFr[i]

                def zb():
                    i = rot["zb"] % 4
                    rot["zb"] += 1
                    return ZB[i], ZBr[i]

                def rmsnorm(T, gcol, out_f32=None):
                    for kc in range(NKC):
                        S.op("act", lambda e, kc=kc: e.activation(out=H[:, kc, 0:T], in_=X[:, kc, 0:T], func=AF.Square),
                             reads=[Xr[kc]], writes=[Hr[kc]])
                    p, pr = psn()
                    S.mm(p[:, 0:T], pr, [(ones_b, H[:, kc, 0:T]) for kc in range(NKC)], reads=Hr + [c_reg])
                    S.op("act", lambda e: e.activation(out=Rs[:, 0:T], in_=p[:, 0:T], func=AF.Sqrt, bias=EPSC[:, 0:1], scale=1.0 / D),
                         reads=[pr, c_reg], writes=[Rr])
                    S.op("dve", lambda e: e.reciprocal(out=Rs[:, 0:T], in_=Rs[:, 0:T]), reads=[Rr], writes=[Rr])
                    for kc in range(NKC):
                        if out_f32 is None:
                            S.op("dve", lambda e, kc=kc: e.scalar_tensor_tensor(
                                out=U[:, kc, 0:T], in0=X[:, kc, 0:T], scalar=PR[:, gcol + kc:gcol + kc + 1], in1=Rs[:, 0:T],
                                op0=ALU.mult, op1=ALU.mult), reads=[Xr[kc], Rr, p_reg], writes=[Ur[kc]])
                        else:
                            S.op("dve", lambda e, kc=kc: e.scalar_tensor_tensor(
                                out=X[:, kc, 0:T], in0=X[:, kc, 0:T], scalar=PR[:, gcol + kc:gcol + kc + 1], in1=Rs[:, 0:T],
                                op0=ALU.mult, op1=ALU.mult), reads=[Xr[kc], Rr, p_reg], writes=[Xr[kc]])

                def ffn(T, wl, bg, bu, bd):
                    for m in range(32):
                        wg, wgr, _ = wget((wl, 4, bg + m))
                        wu, wur, _ = wget((wl, 4, bu + m))
                        pg, pgr = psn()
                        S.mm(pg[:, 0:T], pgr, [(wslice(wg, 4, kc), U[:, kc, 0:T]) for kc in range(NKC)], reads=Ur + [wgr])
                        pu, pur = psn()
                        S.mm(pu[:, 0:T], pur, [(wslice(wu, 4, kc), U[:, kc, 0:T]) for kc in range(NKC)], reads=Ur + [wur])
                        t, tr = tf()
                        S.op("act", lambda e: e.activation(out=t[:, 0:T], in_=pg[:, 0:T], func=AF.Silu), reads=[pgr], writes=[tr])
                        S.op("dve", lambda e: e.tensor_tensor(out=H[:, m, 0:T], in0=pu[:, 0:T], in1=t[:, 0:T], op=ALU.mult),
                             reads=[pur, tr], writes=[Hr[m]])
                    for m in range(32):
                        wd, wdr, _ = wget((wl, 4, bd + m))
                        pd, pdr = psn()
                        S.mm(pd[:, 0:T], pdr, [(wslice(wd, 4, kc), H[:, kc, 0:T]) for kc in range(NKC)], reads=Hr + [wdr])
                        S.op("dve", lambda e: e.scalar_tensor_tensor(out=X[:, m, 0:T], in0=pd[:, 0:T], scalar=0.5, in1=X[:, m, 0:T],
                                                                     op0=ALU.mult, op1=ALU.add), reads=[pdr, Xr[m]], writes=[Xr[m]])

                def phase_c(ti, c0, T, wl):
                    S.dma("sp", H[:, :, 0:T], lg[0:32 * 128, c0:c0 + T].rearrange("(b p) t -> p b t", p=128),
                          reads=lg_reg[ti], writes=Hr)
                    for i in range(4):
                        for blk in range(8):
                            ca, car = CA[rot["ca"] % 2], CAr[rot["ca"] % 2]
                            rot["ca"] += 1
                            for b_ in range(2):
                                src = g2.rearrange("(b i j r) t -> b i r j t", b=2, i=4, j=4)[b_, i, blk * 128:(blk + 1) * 128, :, c0:c0 + T]
                                S.dma("sp", ca[:, 4 * b_:4 * b_ + 4, 0:T], src, reads=[g2_reg], writes=[car])
                            p, pr = psn()
                            S.mm(p[:, 0:T], pr, [(selI(j), ca[:, j, 0:T]) for j in range(8)], reads=[car, c_reg])
                            idx = (4 * i + blk) if blk < 4 else (16 + 4 * i + (blk - 4))
                            S.op("dve", lambda e, idx=idx, p=p: e.tensor_tensor(out=U[:, idx, 0:T], in0=p[:, 0:T], in1=H[:, idx, 0:T],
                                                                              op=ALU.mult), reads=[pr, Hr[idx]], writes=[Ur[idx]])
                    for m in range(32):
                        wa, war, _ = wget((wl, 2, m))
                        wb, wbr, _ = wget((wl, 2, 32 + m))
                        ga, gar = GA[rot["ga"] % 2], GAr[rot["ga"] % 2]
                        rot["ga"] += 1
                        src = lg.rearrange("(g m p) t -> g m p t", g=3, m=32)[1:3, m, :, c0:c0 + T].rearrange("g p t -> p g t")
                        S.dma("sp", ga[:, :, 0:T], src, reads=lg_reg[ti], writes=[gar])
                        pa, par = psn()
                        S.mm(pa[:, 0:T], par, [(wslice(wa, 2, kc), U[:, kc, 0:T]) for kc in range(16)], reads=Ur[0:16] + [war])
                        pb, pbr = psn()
                        S.mm(pb[:, 0:T], pbr, [(wslice(wb, 2, kc), U[:, 16 + kc, 0:T]) for kc in range(16)], reads=Ur[16:32] + [wbr])
                        t1, t1r = tf()
                        t2, t2r = tf()
                        S.op("dve", lambda e: e.tensor_tensor(out=t1[:, 0:T], in0=pa[:, 0:T], in1=ga[:, 0, 0:T], op=ALU.mult),
                             reads=[par, gar], writes=[t1r])
                        S.op("dve", lambda e: e.tensor_tensor(out=t2[:, 0:T], in0=pb[:, 0:T], in1=ga[:, 1, 0:T], op=ALU.mult),
                             reads=[pbr, gar], writes=[t2r])
                        S.op("pool", lambda e: e.tensor_tensor(out=H[:, m, 0:T], in0=t1[:, 0:T], in1=t2[:, 0:T], op=ALU.add),
                             reads=[t1r, t2r], writes=[Hr[m]])
                    for m in range(32):
                        wo, wor, _ = wget((wl, 4, B_WO + m))
                        p, pr = psn()
                        S.mm(p[:, 0:T], pr, [(wslice(wo, 4, kc), H[:, kc, 0:T]) for kc in range(NKC)], reads=Hr + [wor])
                        S.op("dve", lambda e: e.tensor_tensor(out=X[:, m, 0:T], in0=p[:, 0:T], in1=X[:, m, 0:T], op=ALU.add),
                             reads=[pr, Xr[m]], writes=[Xr[m]])
                    rmsnorm(T, PL["nf2"] + 32 * wl)
                    ffn(T, wl, B_F2G, B_F2U, B_F2D)

                def phase_a(ti, c0, T, l):
                    rmsnorm(T, PL["nf1"] + 32 * l)
                    ffn(T, l, B_F1G, B_F1U, B_F1D)
                    rmsnorm(T, PL["nmix"] + 32 * l)

                    def store(z, zr, dst, dreg):
                        nr = Reg()
                        dreg.append(nr)
                        S.dma("sp", dst, z[:, 0:T], reads=[zr], writes=[nr])

                    def sb1_rows(r0):
                        return sb1[r0:r0 + 128, c0:c0 + T]

                    def lg_rows(b):
                        return lg[b * 128:(b + 1) * 128, c0:c0 + T]

                    for s in range(177):
                        w, wr, _ = wget((l, 4, B_WIN + s))
                        p, pr = psn()
                        S.mm(p[:, 0:T], pr, [(wslice(w, 4, kc), U[:, kc, 0:T]) for kc in range(NKC)], reads=Ur + [wr])
                        if s < 16:
                            h = s
                            z, zr = zb()
                            S.op("act", lambda e: e.activation(out=z[:, 0:T], in_=p[:, 0:T], func=AF.Silu), reads=[pr], writes=[zr])
                            store(z, zr, sb1_rows((h // 4) * GRP_ROWS + (h % 4) * 512), sb1_reg)
                        elif s < 32:
                            h = s - 16
                            t1, t1r = tf()
                            S.op("act", lambda e: e.activation(out=t1[:, 0:T], in_=p[:, 0:T], func=AF.Sigmoid), reads=[pr], writes=[t1r])
                            S.op("act", lambda e: e.activation(out=t1[:, 0:T], in_=t1[:, 0:T], func=AF.Ln,
                                                               bias=LB[:, 16 * l + h:16 * l + h + 1],
                                                               scale=LB[:, 16 * L + 16 * l + h:16 * L + 16 * l + h + 1]),
                                 reads=[t1r, p_reg], writes=[t1r])
                            zh, zhr = zb()
                            zl, zlr = zb()
                            S.op("dve", lambda e: e.tensor_copy(out=zh[:, 0:T], in_=t1[:, 0:T]), reads=[t1r], writes=[zhr])
                            S.op("dve", lambda e: e.tensor_tensor(out=zl[:, 0:T], in0=t1[:, 0:T], in1=zh[:, 0:T], op=ALU.subtract),
                                 reads=[t1r, zhr], writes=[zlr])
                            base = (h // 4) * GRP_ROWS + (h % 4) * 512
                            store(zh, zhr, sb1_rows(base + 128), sb1_reg)
                            store(zl, zlr, sb1_rows(base + 256), sb1_reg)
                        elif s < 48:
                            h = s - 32
                            z, zr = zb()
                            S.op("act", lambda e: e.copy(out=z[:, 0:T], in_=p[:, 0:T]), reads=[pr], writes=[zr])
                            store(z, zr, sb1_rows((h // 4) * GRP_ROWS + (h % 4) * 512 + 384), sb1_reg)
                        elif s < 64:
                            h = s - 48
                            z, zr = zb()
                            S.op("act", lambda e: e.activation(out=z[:, 0:T], in_=p[:, 0:T], func=AF.Silu), reads=[pr], writes=[zr])
                            store(z, zr, lg_rows(h), lg_reg[ti])
                        elif s < 80:
                            mh = (s - 64) % 8
                            isk = (s - 64) // 8
                            z, zr = zb()
                            S.op("dve", lambda e: e.tensor_copy(out=z[:, 0:T], in_=p[:, 0:T]), reads=[pr], writes=[zr])
                            store(z, zr, sb1_rows((mh // 2) * GRP_ROWS + 2048 + (mh % 2) * 512 + 128 * isk), sb1_reg)
                        elif s < 96:
                            mh, vb = (s - 80) // 2, (s - 80) % 2
                            z, zr = zb()
                            S.op("dve", lambda e: e.tensor_copy(out=z[:, 0:T], in_=p[:, 0:T]), reads=[pr], writes=[zr])
                            store(z, zr, sb1_rows((mh // 2) * GRP_ROWS + 2048 + (mh % 2) * 512 + 256 + 128 * vb), sb1_reg)
                        elif s < 112:
                            z, zr = zb()
                            S.op("act", lambda e: e.activation(out=z[:, 0:T], in_=p[:, 0:T], func=AF.Sigmoid), reads=[pr], writes=[zr])
                            store(z, zr, lg_rows(16 + (s - 96)), lg_reg[ti])
                        elif s == 112:
                            t1, t1r = tf()
                            t2, t2r = tf()
                            S.op("act", lambda e: e.activation(out=t1[0:16, 0:T], in_=p[0:16, 0:T], func=AF.Tanh,
                                                               bias=GBS[:, l:l + 1], scale=1.0 / 15.0), reads=[pr, p_reg], writes=[t1r])
                            S.op("dve", lambda e: e.tensor_scalar(out=t1[0:16, 0:T], in0=t1[0:16, 0:T], scalar1=15.0, scalar2=None,
                                                                  op0=ALU.mult), reads=[t1r], writes=[t1r])
                            S.op("act", lambda e: e.activation(out=t2[0:16, 0:T], in_=t1[0:16, 0:T], func=AF.Sigmoid), reads=[t1r], writes=[t2r])
                            S.op("act", lambda e: e.activation(out=t2[0:16, 0:T], in_=t2[0:16, 0:T], func=AF.Ln), reads=[t2r], writes=[t2r])
                            for (t, tr, r_lo, off) in ((t1, t1r, 0, 0), (t2, t2r, 8, 4)):
                                zh, zhr = zb()
                                zl, zlr = zb()
                                S.op("dve", lambda e, t=t: e.tensor_copy(out=zh[0:16, 0:T], in_=t[0:16, 0:T]), reads=[tr], writes=[zhr])
                                S.op("dve", lambda e, t=t: e.tensor_tensor(out=zl[0:16, 0:T], in0=t[0:16, 0:T], in1=zh[0:16, 0:T],
                                                                           op=ALU.subtract), reads=[tr, zhr], writes=[zlr])
                                for j in range(4):
                                    r0 = j * GRP_ROWS + 3072 + off
                                    n1, n2 = Reg(), Reg()
                                    sb1_reg.extend([n1, n2])
                                    S.dma("sp", sb1[r0:r0 + 2, c0:c0 + T], zh[r_lo + 2 * j:r_lo + 2 * j + 2, 0:T], reads=[zhr], writes=[n1])
                                    S.dma("sp", sb1[r0 + 2:r0 + 4, c0:c0 + T], zl[r_lo + 2 * j:r_lo + 2 * j + 2, 0:T], reads=[zlr], writes=[n2])
                        else:
                            z, zr = zb()
                            S.op("act", lambda e: e.activation(out=z[:, 0:T], in_=p[:, 0:T], func=AF.Sigmoid), reads=[pr], writes=[zr])
                            store(z, zr, lg_rows(32 + (s - 113)), lg_reg[ti])

                for ti, (c0, T) in enumerate(tiles):
                    S.dma("sp", X[:, :, 0:T], xs[:, :, c0:c0 + T], reads=[xs_reg[ti]], writes=Xr)
                    if l > 0:
                        phase_c(ti, c0, T, l - 1)
                    if l < L:
                        del lg_reg[ti][:]
                        phase_a(ti, c0, T, l)
                        S.dma("sp", xs[:, :, c0:c0 + T], X[:, :, 0:T], reads=Xr, writes=[xs_reg[ti]])
                        if _os.environ.get("MK_DUMP") == "1" and ti > 0:
                            S.dma("sp", out_d[:, :, c0 - NMETA:c0 - NMETA + T], X[:, :, 0:T], reads=Xr, writes=[Reg()])
                    else:
                        rmsnorm(T, PL["nfin"], out_f32=True)
                        if ti > 0:
                            S.dma("sp", out_d[:, :, c0 - NMETA:c0 - NMETA + T], X[:, :, 0:T], reads=Xr, writes=[xs_reg[ti]])
                S.barrier()

        def rec_pass(l):
            with contextlib.ExitStack() as ts:
                def sb(name, shape, dt):
                    return ts.enter_context(nc.sbuf_tensor("%s_%d" % (name, l), shape, dt)), Reg()

                CA, CAr = sb("rCA", [128, 8, 4, 512], BF16)
                GC, GCr = sb("rGC", [8, 8, 512], BF16)
                FT = [sb("rF%d" % i, [128, 512], F32) for i in range(8)]
                BT = [sb("rB%d" % i, [128, 512], BF16) for i in range(5)]
                QP, QPr = sb("rQP", [128, 515], F32)
                KP, KPr = sb("rKP", [128, 515], F32)
                VT, VTr = sb("rVT", [64, 8, 256], BF16)
                AT, ATr = sb("rAT", [64, 64], BF16)
                DT, DTr = sb("rDT", [64, 64], F32)
                KHT, KHTr = sb("rKHT", [64, 128], BF16)
                WJ, WJr = sb("rWJ", [64, 1], F32)
                OB, OBr = sb("rOB", [128, 2, 512], F32)
                ON, ONr = sb("rON", [128, 2, 512], BF16)
                ROW = [sb("rROW%d" % i, [1, 512], F32) for i in range(5)]
                Sf = [sb("rS%d" % i, [128, 128], F32) for i in range(4)]
                Sb = [sb("rSb%d" % i, [128, 128], BF16) for i in range(4)]
                Cf = [sb("rC%d" % i, [128, 256], F32) for i in range(2)]
                Cb = [sb("rCb%d" % i, [128, 256], BF16) for i in range(2)]
                Nf = [sb("rN%d" % i, [128, 1], F32) for i in range(2)]
                Nb = [sb("rNb%d" % i, [128, 128], BF16) for i in range(2)]
                Hq = [sb("rHq%d" % i, [128, 3], F32) for i in range(2)]
                Hk = [sb("rHk%d" % i, [128, 3], F32) for i in range(2)]
                for (t, r) in Sf + Cf + Nf + Hq + Hk:
                    S.op("pool", lambda e, t=t: e.memset(t[:, :], 0.0), writes=[r])
                for (t, r) in Sb + Cb + Nb:
                    S.op("pool", lambda e, t=t: e.memset(t[:, :], 0.0), writes=[r])
                maskT = CT[0:64, C_MASKT:C_MASKT + 64]
                seg = CT[:, C_SEG:C_SEG + 512]

                def rstd_from(p, pr, T, n, dst, dstr):
                    S.op("act", lambda e: e.activation(out=dst[:, 0:T], in_=p[:, 0:T], func=AF.Sqrt, bias=EPSC[:, 0:1], scale=1.0 / n),
                         reads=[pr, c_reg], writes=[dstr])
                    S.op("dve", lambda e: e.reciprocal(out=dst[:, 0:T], in_=dst[:, 0:T]), reads=[dstr], writes=[dstr])

                def hgrn_head(hh, src, c0, T, nch, CH, out_cols):
                    mid, last = CH // 2 - 1, CH - 1
                    for q in range(8):
                        r0 = (4 * (q // 4) + src) * GRP_ROWS + hh * 512
                        S.dma("sp", CA[:, q, :, 0:T], g1t[q % 4][r0:r0 + 512, c0:c0 + T].rearrange("(b p) t -> p b t", p=128),
                              reads=[g1_regs[q % 4]], writes=[CAr])
                    pq, pqr = psn()
                    S.mm(pq[:, 0:T], pqr, [(selI(j), CA[:, j, 0, 0:T]) for j in range(8)], reads=[CAr, c_reg])
                    pl, plr = psn()
                    S.mm(pl[:, 0:T], plr, [(selI(j), CA[:, j, 1, 0:T]) for j in range(8)] +
                         [(selI(j), CA[:, j, 2, 0:T]) for j in range(8)], reads=[CAr, c_reg])
                    for c4 in range(0, nch, 4):
                        pv, pvr = psn()
                        n4 = min(4, nch - c4)
                        for cc in range(n4):
                            c = c4 + cc
                            S.mm(pv[0:CH, cc * 128:(cc + 1) * 128], pvr,
                                 [(CA[:, j, 3, c * CH:(c + 1) * CH], selI(j)) for j in range(8)], reads=[CAr, c_reg])
                        S.op("act", lambda e, pv=pv, c4=c4, n4=n4: e.copy(
                            out=VT[0:CH, c4:c4 + n4, 0:128], in_=pv[0:CH, 0:n4 * 128].rearrange("p (c v) -> p c v", v=128)),
                            reads=[pvr], writes=[VTr])
                    (LF, LFr), (Q, Qr), (B, Br), (NB, NBr), (K1, K1r), (E1, E1r), (E2, E2r), (EB, EBr) = FT
                    (QT, QTr), (KT, KTr), (QH, QHr), (KH, KHr), _ = BT
                    S.op("act", lambda e: e.copy(out=LF[:, 0:T], in_=pl[:, 0:T]), reads=[plr], writes=[LFr])
                    S.op("dve", lambda e: e.tensor_copy(out=Q[:, 0:T], in_=pq[:, 0:T]), reads=[pqr], writes=[Qr])
                    S.op("dve", lambda e: e.tensor_tensor_scan(out=B[:, 0:T], data0=seg[:, 0:T] if CH == 64 else ones_f[:, 0:T],
                                                               data1=LF[:, 0:T], initial=0.0, op0=ALU.mult, op1=ALU.add),
                         reads=[LFr, c_reg], writes=[Br])
                    S.op("act", lambda e: e.activation(out=K1[:, 0:T], in_=LF[:, 0:T], func=AF.Exp), reads=[LFr], writes=[K1r])
                    S.op("dve", lambda e: e.tensor_scalar(out=K1[:, 0:T], in0=K1[:, 0:T], scalar1=-1.0, scalar2=1.0,
                                                          op0=ALU.mult, op1=ALU.add), reads=[K1r], writes=[K1r])
                    S.op("dve", lambda e: e.tensor_scalar(out=NB[:, 0:T], in0=B[:, 0:T], scalar1=-1.0, scalar2=None, op0=ALU.mult),
                         reads=[Br], writes=[NBr])
                    S.op("act", lambda e: e.activation(out=EB[:, 0:T], in_=B[:, 0:T], func=AF.Exp), reads=[Br], writes=[EBr])
                    for c in range(nch):
                        sl = slice(c * CH, (c + 1) * CH)
                        m_, l_ = c * CH + mid, c * CH + last
                        S.op("act", lambda e, sl=sl, m_=m_: e.activation(out=E1[:, sl], in_=B[:, sl], func=AF.Exp,
                                                                         bias=NB[:, m_:m_ + 1], scale=1.0), reads=[Br, NBr], writes=[E1r])
                        S.op("act", lambda e, sl=sl, m_=m_: e.activation(out=E2[:, sl], in_=B[:, sl], func=AF.Exp,
                                                                         bias=B[:, m_:m_ + 1], scale=-1.0), reads=[Br], writes=[E2r])
                    S.op("dve", lambda e: e.tensor_tensor(out=QT[:, 0:T], in0=Q[:, 0:T], in1=E1[:, 0:T], op=ALU.mult),
                         reads=[Qr, E1r], writes=[QTr])
                    S.op("dve", lambda e: e.tensor_tensor(out=KT[:, 0:T], in0=K1[:, 0:T], in1=E2[:, 0:T], op=ALU.mult),
                         reads=[K1r, E2r], writes=[KTr])
                    S.op("dve", lambda e: e.tensor_tensor(out=QH[:, 0:T], in0=Q[:, 0:T], in1=EB[:, 0:T], op=ALU.mult),
                         reads=[Qr, EBr], writes=[QHr])
                    for c in range(nch):
                        sl = slice(c * CH, (c + 1) * CH)
                        l_ = c * CH + last
                        S.op("act", lambda e, sl=sl, l_=l_: e.activation(out=E1[:, sl], in_=B[:, sl], func=AF.Exp,
                                                                         bias=B[:, l_:l_ + 1], scale=-1.0), reads=[Br, QTr], writes=[E1r])
                    S.op("dve", lambda e: e.tensor_tensor(out=KH[:, 0:T], in0=K1[:, 0:T], in1=E1[:, 0:T], op=ALU.mult),
                         reads=[K1r, E1r], writes=[KHr])
                    sf, sfr = Sf[hh]
                    sbb, sbr = Sb[hh]
                    for c in range(nch):
                        sl = slice(c * CH, (c + 1) * CH)
                        l_ = c * CH + last
                        pa, par = psn()
                        S.mm(pa[0:CH, 0:CH], par, [(KT[:, sl], QT[:, sl])], reads=[KTr, QTr])
                        S.op("dve", lambda e, pa=pa: e.tensor_tensor(out=AT[0:CH, 0:CH], in0=pa[0:CH, 0:CH], in1=maskT[0:CH, 0:CH], op=ALU.mult),
                             reads=[par, c_reg], writes=[ATr])
                        S.transpose(PT[0:CH, 0:128], PT_reg, KH[:, sl], ident, reads=[KHr, c_reg])
                        S.op("act", lambda e: e.copy(out=KHT[0:CH, :], in_=PT[0:CH, 0:128]), reads=[PT_reg], writes=[KHTr])
                        po, por = psn()
                        S.mm(po[:, 0:CH], por, [(VT[0:CH, c, 0:128], AT[0:CH, 0:CH]), (sbb[:, :], QH[:, sl])],
                             reads=[VTr, ATr, sbr, QHr])
                        S.op("act", lambda e, po=po, sl=sl: e.copy(out=OB[:, 0, sl], in_=po[:, 0:CH]), reads=[por], writes=[OBr])
                        pd, pdr = psn()
                        S.mm(pd[:, 0:128], pdr, [(KHT[0:CH, :], VT[0:CH, c, 0:128])], reads=[KHTr, VTr])
                        S.op("dve", lambda e, pd=pd, l_=l_: e.scalar_tensor_tensor(out=sf[:, :], in0=sf[:, :], scalar=EB[:, l_:l_ + 1],
                                                                                   in1=pd[:, 0:128], op0=ALU.mult, op1=ALU.add),
                             reads=[sfr, EBr, pdr], writes=[sfr])
                        S.op("act", lambda e: e.copy(out=sbb[:, :], in_=sf[:, :]), reads=[sfr], writes=[sbr])
                    S.op("act", lambda e: e.activation(out=ON[:, 0, 0:T], in_=OB[:, 0, 0:T], func=AF.Square), reads=[OBr], writes=[ONr])
                    ps_, psr = psn()
                    S.mm(ps_[:, 0:T], psr, [(ones_b, ON[:, 0, 0:T])], reads=[ONr, c_reg])
                    rstd_from(ps_, psr, T, 128.0, E2, E2r)
                    gcol = PL["hgn"] + 4 * l + hh
                    S.op("dve", lambda e: e.scalar_tensor_tensor(out=ON[:, 1, 0:T], in0=OB[:, 0, 0:T], scalar=PR[:, gcol:gcol + 1],
                                                                 in1=E2[:, 0:T], op0=ALU.mult, op1=ALU.mult),
                         reads=[OBr, E2r, p_reg, ONr], writes=[ONr])
                    for (dj, dc0) in out_cols:
                        r0 = dj * SB2_ROWS + hh * 128
                        nr = Reg()
                        sb2_reg.append(nr)
                        S.dma("sp", sb2[r0:r0 + 128, dc0:dc0 + T], ON[:, 1, 0:T], reads=[ONr], writes=[nr])

                def mlstm_head(mm_, src, c0, T, nch, CH, out_cols):
                    last = CH - 1
                    for q in range(8):
                        r0 = (4 * (q // 4) + src) * GRP_ROWS + 2048 + mm_ * 512
                        S.dma("sp", CA[:, q, :, 0:T], g1t[q % 4][r0:r0 + 512, c0:c0 + T].rearrange("(b p) t -> p b t", p=128),
                              reads=[g1_regs[q % 4]], writes=[CAr])
                        rg = (4 * (q // 4) + src) * GRP_ROWS + 3072
                        S.dma("sp", GC[:, q, 0:T], g1t[q % 4][rg:rg + 8, c0:c0 + T], reads=[g1_regs[q % 4]], writes=[GCr])
                    (F1, F1r), (F2, F2r), (QC, QCr), (KC, KCr), (EF, EFr), (F6, F6r), (F7, F7r), (F8, F8r) = FT
                    (QS, QSr), (KS, KSr), (QH, QHr), (B4, B4r), (B5, B5r) = BT
                    (RI, RIr), (RF, RFr), (RA, RAr), (RW, RWr), _ = ROW
                    for (blk, P_, Pr_, Hh) in ((0, QP, QPr, Hq[mm_]), (1, KP, KPr, Hk[mm_])):
                        p, pr = psn()
                        S.mm(p[:, 0:T], pr, [(selI(j), CA[:, j, blk, 0:T]) for j in range(8)], reads=[CAr, c_reg])
                        S.op("dve", lambda e, P_=P_, Hh=Hh: e.tensor_copy(out=P_[:, 0:3], in_=Hh[0][:, 0:3]), reads=[Hh[1]], writes=[Pr_])
                        S.op("act", lambda e, P_=P_, p=p: e.copy(out=P_[:, 3:3 + T], in_=p[:, 0:T]), reads=[pr], writes=[Pr_])
                        S.op("dve", lambda e, P_=P_, Hh=Hh: e.tensor_copy(out=Hh[0][:, 0:3], in_=P_[:, T:T + 3]), reads=[Pr_], writes=[Hh[1]])
                    pg, pgr = psn()
                    S.mm(pg[0:1, 0:T], pgr, [(CB[0:8, C_SELG + 4 * j + mm_:C_SELG + 4 * j + mm_ + 1], GC[:, j, 0:T]) for j in range(8)],
                         reads=[GCr, c_reg])
                    S.op("act", lambda e: e.copy(out=RI[:, 0:T], in_=pg[0:1, 0:T]), reads=[pgr], writes=[RIr])
                    pg2, pg2r = psn()
                    S.mm(pg2[0:1, 0:T], pg2r, [(CB[0:8, C_SELG + 4 * j + 2 + mm_:C_SELG + 4 * j + 2 + mm_ + 1], GC[:, j, 0:T]) for j in range(8)],
                         reads=[GCr, c_reg])
                    S.op("act", lambda e: e.copy(out=RF[:, 0:T], in_=pg2[0:1, 0:T]), reads=[pg2r], writes=[RFr])
                    S.op("dve", lambda e: e.tensor_tensor_scan(out=RF[:, 0:T], data0=seg[0:1, 0:T] if CH == 64 else ones_f[0:1, 0:T],
                                                               data1=RF[:, 0:T], initial=0.0, op0=ALU.mult, op1=ALU.add),
                         reads=[RFr, c_reg], writes=[RFr])
                    S.op("dve", lambda e: e.tensor_tensor(out=RA[:, 0:T], in0=RI[:, 0:T], in1=RF[:, 0:T], op=ALU.subtract),
                         reads=[RIr, RFr], writes=[RAr])
                    pf, pfr = psn()
                    S.mm(pf[:, 0:T], pfr, [(ones_f[0:1, 0:128], RF[:, 0:T])], reads=[RFr, c_reg])
                    S.op("act", lambda e: e.activation(out=EF[:, 0:T], in_=pf[:, 0:T], func=AF.Exp), reads=[pfr], writes=[EFr])
                    cwc = PL["cw"] + 8 * l
                    cbc = PL["cb"] + 2 * l
                    for (isk, P_, Pr_, C_, Cr_) in ((0, QP, QPr, QC, QCr), (1, KP, KPr, KC, KCr)):
                        col = lambda w: PL["cw"] + (l * 2 + mm_) * 8 + isk * 4 + w
                        bcol = PL["cb"] + (l * 2 + mm_) * 2 + isk
                        S.op("dve", lambda e, P_=P_, C_=C_, col=col, bcol=bcol: e.tensor_scalar(
                            out=C_[:, 0:T], in0=P_[:, 0:T], scalar1=PR[:, col(0):col(0) + 1], scalar2=PR[:, bcol:bcol + 1],
                            op0=ALU.mult, op1=ALU.add), reads=[Pr_, p_reg], writes=[Cr_])
                        for w in (1, 2, 3):
                            S.op("dve", lambda e, P_=P_, C_=C_, col=col, w=w: e.scalar_tensor_tensor(
                                out=C_[:, 0:T], in0=P_[:, w:w + T], scalar=PR[:, col(w):col(w) + 1], in1=C_[:, 0:T],
                                op0=ALU.mult, op1=ALU.add), reads=[Pr_, p_reg, Cr_], writes=[Cr_])
                    S.op("act", lambda e: e.activation(out=QS[:, 0:T], in_=QC[:, 0:T], func=AF.Silu), reads=[QCr], writes=[QSr])
                    S.op("act", lambda e: e.activation(out=KC[:, 0:T], in_=KC[:, 0:T], func=AF.Silu), reads=[KCr], writes=[KCr])
                    S.op("dve", lambda e: e.tensor_scalar(out=KS[:, 0:T], in0=KC[:, 0:T], scalar1=128.0 ** -0.5, scalar2=None, op0=ALU.mult),
                         reads=[KCr], writes=[KSr])
                    S.op("dve", lambda e: e.tensor_tensor(out=QH[:, 0:T], in0=QS[:, 0:T], in1=EF[:, 0:T], op=ALU.mult),
                         reads=[QSr, EFr], writes=[QHr])
                    for c2 in range(0, nch, 2):
                        pv, pvr = psn()
                        n2 = min(2, nch - c2)
                        for cc in range(n2):
                            c = c2 + cc
                            for vb in range(2):
                                S.mm(pv[0:CH, cc * 256 + vb * 128:cc * 256 + (vb + 1) * 128], pvr,
                                     [(CA[:, j, 2 + vb, c * CH:(c + 1) * CH], selI(j)) for j in range(8)], reads=[CAr, c_reg])
                        S.op("act", lambda e, pv=pv, c2=c2, n2=n2: e.copy(
                            out=VT[0:CH, c2:c2 + n2, :], in_=pv[0:CH, 0:n2 * 256].rearrange("p (c v) -> p c v", v=256)),
                            reads=[pvr], writes=[VTr])
                    cf, cfr = Cf[mm_]
                    cb_, cbr = Cb[mm_]
                    nf, nfr = Nf[mm_]
                    nb, nbr = Nb[mm_]
                    for c in range(nch):
                        sl = slice(c * CH, (c + 1) * CH)
                        l_ = c * CH + last
                        pD, pDr = psn()
                        S.mm(pD[0:CH, 0:CH], pDr, [(RA[0:1, sl], ones_f[0:1, 0:CH]), (ones_f[0:1, 0:CH], RF[0:1, sl])],
                             reads=[RAr, RFr, c_reg])
                        S.op("act", lambda e, pD=pD: e.activation(out=DT[0:CH, 0:CH], in_=pD[0:CH, 0:CH], func=AF.Exp), reads=[pDr], writes=[DTr])
                        S.op("pool", lambda e: e.tensor_tensor(out=DT[0:CH, 0:CH], in0=DT[0:CH, 0:CH], in1=maskT[0:CH, 0:CH], op=ALU.mult),
                             reads=[DTr, c_reg], writes=[DTr])
                        pS, pSr = psn()
                        S.mm(pS[0:CH, 0:CH], pSr, [(KS[:, sl], QS[:, sl])], reads=[KSr, QSr])
                        S.op("dve", lambda e, pS=pS: e.tensor_tensor(out=AT[0:CH, 0:CH], in0=pS[0:CH, 0:CH], in1=DT[0:CH, 0:CH], op=ALU.mult),
                             reads=[pSr, DTr], writes=[ATr])
                        pden, pdenr = psn()
                        S.mm(pden[:, 0:CH], pdenr, [(ones_b[0:CH, :], AT[0:CH, 0:CH]), (nb[:, :], QH[:, sl])],
                             reads=[ATr, nbr, QHr, c_reg])
                        S.op("act", lambda e, pden=pden: e.activation(out=F6[:, 0:CH], in_=pden[:, 0:CH], func=AF.Abs), reads=[pdenr], writes=[F6r])
                        S.op("dve", lambda e: e.tensor_scalar(out=F6[:, 0:CH], in0=F6[:, 0:CH], scalar1=1.0, scalar2=None, op0=ALU.max),
                             reads=[F6r], writes=[F6r])
                        S.op("dve", lambda e: e.reciprocal(out=F6[:, 0:CH], in_=F6[:, 0:CH]), reads=[F6r], writes=[F6r])
                        for vb in range(2):
                            pn, pnr = psn()
                            S.mm(pn[:, 0:CH], pnr, [(VT[0:CH, c, vb * 128:(vb + 1) * 128], AT[0:CH, 0:CH]),
                                                    (cb_[:, vb * 128:(vb + 1) * 128], QH[:, sl])], reads=[VTr, ATr, cbr, QHr])
                            S.op("dve", lambda e, pn=pn, vb=vb, sl=sl: e.tensor_tensor(out=OB[:, vb, sl], in0=pn[:, 0:CH], in1=F6[:, 0:CH],
                                                                                     op=ALU.mult), reads=[pnr, F6r], writes=[OBr])
                        pw, pwr = psn()
                        S.mm(pw[0:CH, 0:1], pwr, [(RA[0:1, sl], ones_f[0:1, 0:1]), (ones_f[0:1, 0:CH], RF[0:1, l_:l_ + 1])],
                             reads=[RAr, RFr, c_reg])
                        S.op("act", lambda e, pw=pw: e.activation(out=WJ[0:CH, :], in_=pw[0:CH, 0:1], func=AF.Exp), reads=[pwr], writes=[WJr])
                        S.transpose(PT[0:CH, 0:128], PT_reg, KS[:, sl], ident, reads=[KSr, c_reg])
                        S.op("dve", lambda e: e.tensor_scalar(out=KHT[0:CH, :], in0=PT[0:CH, 0:128], scalar1=WJ[0:CH, 0:1], scalar2=None,
                                                              op0=ALU.mult), reads=[PT_reg, WJr], writes=[KHTr])
                        pc, pcr = psn()
                        S.mm(pc[:, 0:256], pcr, [(KHT[0:CH, :], VT[0:CH, c, :])], reads=[KHTr, VTr])
                        S.op("dve", lambda e, pc=pc, l_=l_: e.scalar_tensor_tensor(out=cf[:, :], in0=cf[:, :], scalar=EF[:, l_:l_ + 1],
                                                                                   in1=pc[:, 0:256], op0=ALU.mult, op1=ALU.add),
                             reads=[cfr, EFr, pcr], writes=[cfr])
                        S.op("act", lambda e: e.copy(out=cb_[:, :], in_=cf[:, :]), reads=[cfr], writes=[cbr])
                        pnn, pnnr = psn()
                        S.mm(pnn[:, 0:1], pnnr, [(KHT[0:CH, :], ones_b[0:CH, 0:1])], reads=[KHTr, c_reg])
                        S.op("dve", lambda e, pnn=pnn, l_=l_: e.scalar_tensor_tensor(out=nf[:, :], in0=nf[:, :], scalar=EF[:, l_:l_ + 1],
                                                                                     in1=pnn[:, 0:1], op0=ALU.mult, op1=ALU.add),
                             reads=[nfr, EFr, pnnr], writes=[nfr])
                        S.op("dve", lambda e: e.tensor_scalar(out=nb[:, :], in0=ones_f[:, 0:128], scalar1=nf[:, 0:1], scalar2=None, op0=ALU.mult),
                             reads=[nfr, c_reg], writes=[nbr])
                    S.op("act", lambda e: e.activation(out=ON[:, :, 0:T], in_=OB[:, :, 0:T], func=AF.Square), reads=[OBr], writes=[ONr])
                    ps_, psr = psn()
                    S.mm(ps_[:, 0:T], psr, [(ones_b, ON[:, 0, 0:T]), (ones_b, ON[:, 1, 0:T])], reads=[ONr, c_reg])
                    rstd_from(ps_, psr, T, 256.0, F7, F7r)
                    for vb in range(2):
                        gcol = PL["mln"] + 4 * l + 2 * mm_ + vb
                        S.op("dve", lambda e, vb=vb, gcol=gcol: e.scalar_tensor_tensor(
                            out=ON[:, vb, 0:T], in0=OB[:, vb, 0:T], scalar=PR[:, gcol:gcol + 1], in1=F7[:, 0:T],
                            op0=ALU.mult, op1=ALU.mult), reads=[OBr, F7r, p_reg, ONr], writes=[ONr])
                    for (dj, dc0) in out_cols:
                        for vb in range(2):
                            r0 = dj * SB2_ROWS + 512 + mm_ * 256 + vb * 128
                            nr = Reg()
                            sb2_reg.append(nr)
                            S.dma("sp", sb2[r0:r0 + 128, dc0:dc0 + T], ON[:, vb, 0:T], reads=[ONr], writes=[nr])

                seq = [(0, 0, NMETA, 1, NMETA, [(dj, 0) for dj in range(4)])]
                for t in range(NSEQT):
                    src, lt = t // NT, t % NT
                    seq.append((src, NMETA + 512 * lt, 512, 8, 64, [(src, NMETA + 512 * lt)]))
                for (src, c0, T, nch, CH, oc) in seq:
                    for hh in range(4):
                        hgrn_head(hh, src, c0, T, nch, CH, oc)
                    for mm_ in range(2):
                        mlstm_head(mm_, src, c0, T, nch, CH, oc)
                S.barrier()

        stop = int(_os.environ.get("MK_STOP", "99"))
        for l in range(L + 1):
            if stop >= 2 + 4 * l:
                token_pass(l)
            if l < L:
                if stop >= 3 + 4 * l:
                    for j in range(4):
                        S.allgather(GROUPS8, sb1[j * GRP_ROWS:(j + 1) * GRP_ROWS, :], g1t[j][:, :], reads=list(sb1_reg), writes=[g1_regs[j]])
                del sb1_reg[:]
                if stop >= 4 + 4 * l:
                    rec_pass(l)
                if stop >= 5 + 4 * l:
                    S.allgather(GROUPS8, sb2[:, :], g2[:, :], reads=list(sb2_reg), writes=[g2_reg])
                del sb2_reg[:]
        S.barrier()
        assert stop < 99 or wst["used"] == len(plan)
    return nc


def _slabify(W, rank, nranks=8):
    K, N = W.shape
    nkl = K // 128 // nranks
    Wr = W[rank * nkl * 128:(rank + 1) * nkl * 128, :].reshape(nkl, 128, N // 128, 128)
    return np.ascontiguousarray(Wr.transpose(2, 1, 0, 3)).reshape(N // 128 * 128, nkl * 128)


def _fm(v, n):
    return np.ascontiguousarray(np.asarray(v, np.float32).reshape(n, 128).T)


def _consts(rank8):
    c = np.zeros((128, NCONST), np.float32)
    c[:, C_ID:C_ID + 128] = np.eye(128, dtype=np.float32)
    c[:, C_SEL + 128 * rank8:C_SEL + 128 * (rank8 + 1)] = np.eye(128, dtype=np.float32)
    jj, ii = np.meshgrid(np.arange(64), np.arange(64), indexing="ij")
    c[0:64, C_MASKT:C_MASKT + 64] = (jj <= ii).astype(np.float32)
    seg = np.ones(512, np.float32)
    seg[::64] = 0.0
    c[:, C_SEG:C_SEG + 512] = seg[None, :]
    c[:, C_ONES:C_ONES + 128] = 1.0
    for w, rows in enumerate(((0, 2), (1, 3), (4, 6), (5, 7))):
        for r in rows:
            c[r, C_SELG + 4 * rank8 + w] = 1.0
    return c


def kernel(x, meta_tokens, hgrn_lb_logits, norm_ffn1, ffn1_w_gate, ffn1_w_up, ffn1_w_down,
           norm_mix, w_in, mlstm_conv_w, mlstm_conv_b, mlstm_igate_b, mlstm_fgate_b,
           hgrn_out_norm, mlstm_out_norm, w_branch_a, w_branch_b, w_out,
           norm_ffn2, ffn2_w_gate, ffn2_w_up, ffn2_w_down, final_norm):
    x = np.asarray(x)
    Bsz, SEQ, _ = x.shape
    L = np.asarray(w_in).shape[0]
    assert Bsz == 2 and SEQ % 2048 == 0
    TOK = SEQ // 4
    NT = TOK // 512
    PL = prm_layout(L)
    cols = []
    offs = np.cumsum([0, 2048, 2048, 2048, 2048, 1024, 1024, 2048, 2048, 8, 8, 4096, 4096])
    NIN = int(offs[-1])
    for g in (0, 1, 2, 3, 4, 5, 6, 7):
        cols.append(np.arange(offs[g], offs[g + 1]))
    cols.append(np.concatenate([np.arange(offs[8], offs[10]), np.arange(NIN, NIN + 112)]))
    cols.append(np.arange(offs[10], offs[11]))
    cols.append(np.arange(offs[11], offs[12]))
    cols = np.concatenate(cols)
    in_maps = [dict() for _ in range(8)]
    for l in range(L):
        win_ext = np.concatenate([np.asarray(w_in[l]), np.zeros((D, 112), np.float32)], axis=1)[:, cols]
        mats4 = [np.asarray(ffn1_w_gate[l]), np.asarray(ffn1_w_up[l]), np.asarray(ffn1_w_down[l]), win_ext,
                 np.asarray(w_out[l]), np.asarray(ffn2_w_gate[l]), np.asarray(ffn2_w_up[l]), np.asarray(ffn2_w_down[l])]
        mats2 = [np.asarray(w_branch_a[l]), np.asarray(w_branch_b[l])]
        for c in range(8):
            in_maps[c]["w4_%d" % l] = np.concatenate([_slabify(m, c) for m in mats4], axis=0)
            in_maps[c]["w2_%d" % l] = np.concatenate([_slabify(m, c) for m in mats2], axis=0)
        del win_ext, mats4, mats2
    metaT = np.ascontiguousarray(np.asarray(meta_tokens, np.float32).T).reshape(NKC, 128, NMETA).transpose(1, 0, 2)
    for c in range(8):
        b, r4 = c // 4, c % 4
        xc = x[b, r4 * TOK:(r4 + 1) * TOK, :]
        xT = np.ascontiguousarray(xc.T).reshape(NKC, 128, TOK).transpose(1, 0, 2)
        in_maps[c]["xin"] = np.ascontiguousarray(np.concatenate([metaT, xT], axis=2))
        in_maps[c]["consts"] = _consts(c)
        p = np.zeros((128, PL["_n"]), np.float32)
        for l in range(L):
            p[:, PL["nf1"] + 32 * l:PL["nf1"] + 32 * (l + 1)] = _fm(norm_ffn1[l], 32)
            p[:, PL["nmix"] + 32 * l:PL["nmix"] + 32 * (l + 1)] = _fm(norm_mix[l], 32)
            p[:, PL["nf2"] + 32 * l:PL["nf2"] + 32 * (l + 1)] = _fm(norm_ffn2[l], 32)
            p[:, PL["lbl"] + 16 * l:PL["lbl"] + 16 * (l + 1)] = _fm(hgrn_lb_logits[l], 16)
            cw = np.asarray(mlstm_conv_w[l], np.float32)
            cb = np.asarray(mlstm_conv_b[l], np.float32)
            for mm_ in range(2):
                mh = 2 * r4 + mm_
                for isk in range(2):
                    ch0 = isk * 1024 + mh * 128
                    for w in range(4):
                        p[:, PL["cw"] + (l * 2 + mm_) * 8 + isk * 4 + w] = cw[w, ch0:ch0 + 128]
                    p[:, PL["cb"] + (l * 2 + mm_) * 2 + isk] = cb[ch0:ch0 + 128]
            p[0:8, PL["gb"] + l] = np.asarray(mlstm_igate_b[l], np.float32)
            p[8:16, PL["gb"] + l] = np.asarray(mlstm_fgate_b[l], np.float32)
            hg = _fm(hgrn_out_norm[l], 16)
            p[:, PL["hgn"] + 4 * l:PL["hgn"] + 4 * (l + 1)] = hg[:, 4 * r4:4 * r4 + 4]
            ml = _fm(mlstm_out_norm[l], 16)
            p[:, PL["mln"] + 4 * l:PL["mln"] + 4 * (l + 1)] = ml[:, 4 * r4:4 * r4 + 4]
        p[:, PL["nfin"]:PL["nfin"] + 32] = _fm(final_norm, 32)
        in_maps[c]["prm"] = p
    nc = build_nc(L, NT)
    res = run_bass_kernel_spmd(nc, in_maps, core_ids=list(range(8)))
    out = np.empty((Bsz, SEQ, D), np.float32)
    for c in range(8):
        b, r4 = c // 4, c % 4
        o = np.asarray(res.results[c]["out"])
        out[b, r4 * TOK:(r4 + 1) * TOK, :] = o.transpose(2, 1, 0).reshape(TOK, D)
    return out
```
